# Optimizing a Trainium2 kernel written in Bass

```python
import jax, jax.numpy as jnp
from jax import lax
import numpy as np

D_MODEL = 1024
BATCH = 16
SEQ = 4096
DEPTH = 4

GRID_W = 64
N_EVEN = (DEPTH + 1) // 2
N_ODD = DEPTH // 2
REC_HEAD_DIM = 128
A_WIDTH = D_MODEL // 2
A_HEADS = A_WIDTH // REC_HEAD_DIM
A_DK = REC_HEAD_DIM
A_KEY = A_HEADS * A_DK
B_WIDTH = D_MODEL - A_WIDTH
B_HEADS = B_WIDTH // REC_HEAD_DIM
B_DH = REC_HEAD_DIM
MIX_WIDTH = A_WIDTH + B_WIDTH
_IN_SIZES = (A_KEY, A_KEY, A_KEY, A_WIDTH, A_WIDTH,
             B_WIDTH, B_WIDTH, B_WIDTH, B_WIDTH, 4 * B_HEADS)
IN_COLS = 3 * A_KEY + 2 * A_WIDTH + 4 * B_WIDTH + 4 * B_HEADS
CHUNK = 64
CONV_W = 5
NA_DH = 32
NA_HEADS = D_MODEL // NA_DH
WIN_R = 8
WIN_C = 16
COL_BLOCK = 16
COL_BAND = COL_BLOCK + WIN_C
FFN_HIDDEN = -(-8 * D_MODEL // (3 * 256)) * 256
ALPHA = (2.0 * DEPTH) ** 0.25
BETA = (8.0 * DEPTH) ** -0.25
LN_EPS = 1e-5
GN_EPS = 1e-6
NEG_BIG = -1e30
LB_FLOOR = 1e-30

kernel_name = "hgrn2_mlstm_natten_hybrid_encoder"


def _heads(t, n_heads):
    b, s, _ = t.shape
    return t.reshape(b, s, n_heads, -1).transpose(0, 2, 1, 3)


def _merge(t):
    b, n, s, d = t.shape
    return t.transpose(0, 2, 1, 3).reshape(b, s, n * d)


def _to_chunks(t):
    b, h, s = t.shape[:3]
    return jnp.moveaxis(t.reshape(b, h, s // CHUNK, CHUNK, *t.shape[3:]), 2, 0)


def _from_chunks(t):
    t = jnp.moveaxis(t, 0, 2)
    return t.reshape(t.shape[0], t.shape[1], -1, *t.shape[4:])


def _flip(t):
    return jnp.flip(t, axis=2)


def _bidirectional(scan_fn, fwd_args, bwd_args):
    return scan_fn(*fwd_args) + _flip(scan_fn(*[_flip(a) for a in bwd_args]))


def _hgrn2_scan(q, k, v, log_f):
    bsz, nh, _, dk = q.shape
    dv = v.shape[-1]
    mask = jnp.tril(jnp.ones((CHUNK, CHUNK), bool))[:, :, None]

    def step(state, xs):
        qc, kc, vc, lfc = xs
        b = jnp.cumsum(lfc, axis=2)
        rel = b[:, :, :, None, :] - b[:, :, None, :, :]
        decay = jnp.exp(jnp.where(mask, rel, NEG_BIG))
        scores = jnp.einsum('bhtd,bhsd,bhtsd->bhts', qc, kc, decay)
        o = (jnp.einsum('bhts,bhsv->bhtv', scores, vc)
             + jnp.einsum('bhtd,bhdv->bhtv', qc * jnp.exp(b), state))
        b_last = b[:, :, -1:, :]
        state = (jnp.exp(b_last[:, :, 0, :])[..., None] * state
                 + jnp.einsum('bhsd,bhsv->bhdv', kc * jnp.exp(b_last - b), vc))
        return state, o

    s0 = jnp.zeros((bsz, nh, dk, dv), jnp.float32)
    _, o = lax.scan(step, s0, (_to_chunks(q), _to_chunks(k), _to_chunks(v), _to_chunks(log_f)))
    return _from_chunks(o)


def _mlstm_scan(q, k, v, log_i, log_f):
    bsz, nh, _, dh = q.shape
    mask = jnp.tril(jnp.ones((CHUNK, CHUNK), bool))

    def step(carry, xs):
        c_st, n_st, m_st = carry
        qc, kc, vc, ic, fc = xs
        b = jnp.cumsum(fc, axis=-1)
        d_intra = jnp.where(mask, b[..., :, None] - b[..., None, :] + ic[..., None, :], NEG_BIG)
        d_inter = b + m_st[..., None]
        m_t = jnp.maximum(d_inter, jnp.max(d_intra, axis=-1))
        w_intra = jnp.exp(d_intra - m_t[..., None])
        w_inter = jnp.exp(d_inter - m_t)
        qk = jnp.einsum('bhtd,bhsd->bhts', qc, kc) * w_intra
        num = (jnp.einsum('bhts,bhsv->bhtv', qk, vc)
               + w_inter[..., None] * jnp.einsum('bhtd,bhdv->bhtv', qc, c_st))
        den = jnp.sum(qk, axis=-1) + w_inter * jnp.einsum('bhtd,bhd->bht', qc, n_st)
        h = num / jnp.maximum(jnp.abs(den), jnp.exp(-m_t))[..., None]
        b_last = b[..., -1]
        g = b_last[..., None] - b + ic
        m_new = jnp.maximum(b_last + m_st, jnp.max(g, axis=-1))
        w_old = jnp.exp(b_last + m_st - m_new)
        w_s = jnp.exp(g - m_new[..., None])
        c_st = w_old[..., None, None] * c_st + jnp.einsum('bhs,bhsd,bhsv->bhdv', w_s, kc, vc)
        n_st = w_old[..., None] * n_st + jnp.einsum('bhs,bhsd->bhd', w_s, kc)
        return (c_st, n_st, m_new), h

    carry0 = (jnp.zeros((bsz, nh, dh, dh), jnp.float32),
              jnp.zeros((bsz, nh, dh), jnp.float32),
              jnp.zeros((bsz, nh), jnp.float32))
    _, h = lax.scan(step, carry0, (_to_chunks(q), _to_chunks(k), _to_chunks(v),
                                   _to_chunks(log_i), _to_chunks(log_f)))
    return _from_chunks(h)


def _centred_dwconv(t, w):
    pad = CONV_W // 2
    s = t.shape[1]
    tp = jnp.pad(t, ((0, 0), (pad, pad), (0, 0)))
    out = tp[:, 0:s] * w[0]
    for j in range(1, CONV_W):
        out = out + tp[:, j:j + s] * w[j]
    return out


def _even_mixer(x, w_in, gate_bias, lb_fwd, lb_bwd, conv_w, gn_a, gn_b, w_out):
    f32 = jnp.float32
    proj = x @ w_in
    (aq, af_f, af_b, ai, ag, bq, bk, bv, bo, bg) = jnp.split(
        proj, np.cumsum(_IN_SIZES)[:-1].tolist(), axis=-1)

    q_a = _heads(aq, A_HEADS).astype(f32)
    v_a = _heads(ai, A_HEADS).astype(f32)

    def log_forget(z, lb):
        lb = lb.reshape(A_HEADS, 1, A_DK)
        return jnp.logaddexp(jnp.log(jnp.maximum(lb, LB_FLOOR)), jnp.log1p(-lb)
                             + jax.nn.log_sigmoid(_heads(z, A_HEADS).astype(f32)))

    lf_f = log_forget(af_f, lb_fwd)
    lf_b = log_forget(af_b, lb_bwd)
    o_a = _bidirectional(_hgrn2_scan, (q_a, -jnp.expm1(lf_f), v_a, lf_f),
                         (q_a, -jnp.expm1(lf_b), v_a, lf_b))
    o_a = o_a * lax.rsqrt(jnp.mean(jnp.square(o_a), -1, keepdims=True) + GN_EPS)
    o_a = _merge(o_a) * gn_a * jax.nn.silu(ag.astype(f32))

    qk = jax.nn.silu(_centred_dwconv(jnp.concatenate([bq, bk], -1), conv_w))
    q_b = _heads(qk[..., :B_WIDTH], B_HEADS).astype(f32)
    k_b = _heads(qk[..., B_WIDTH:], B_HEADS).astype(f32) * (B_DH ** -0.5)
    v_b = _heads(bv, B_HEADS).astype(f32)
    gates = (bg + gate_bias).astype(f32)
    gates = gates.reshape(gates.shape[0], gates.shape[1], 4, B_HEADS).transpose(2, 0, 3, 1)
    li_f, li_b = gates[0], gates[1]
    lfm_f, lfm_b = jax.nn.log_sigmoid(gates[2]), jax.nn.log_sigmoid(gates[3])
    h_b = _bidirectional(_mlstm_scan, (q_b, k_b, v_b, li_f, lfm_f),
                         (q_b, k_b, v_b, li_b, lfm_b))
    mu = jnp.mean(h_b, -1, keepdims=True)
    h_b = (h_b - mu) * lax.rsqrt(jnp.mean(jnp.square(h_b - mu), -1, keepdims=True) + GN_EPS)
    h_b = _merge(h_b) * gn_b * jax.nn.sigmoid(bo.astype(f32))

    return jnp.concatenate([o_a, h_b], -1).astype(x.dtype) @ w_out


def _neighbourhood_attention(x, w_qkv, rpb, w_out):
    bsz, seq, _ = x.shape
    rows = seq // GRID_W
    wr = min(WIN_R, rows)
    qkv = (x @ w_qkv).reshape(bsz, rows, GRID_W, 3, NA_HEADS, NA_DH)
    qkv = qkv.transpose(3, 0, 4, 1, 2, 5)
    q, k, v = qkv[0] * (NA_DH ** -0.5), qkv[1], qkv[2]

    n_cb = GRID_W // COL_BLOCK
    qcol = np.arange(GRID_W).reshape(n_cb, COL_BLOCK)
    c0 = np.clip(qcol - WIN_C // 2, 0, GRID_W - WIN_C)
    band = np.clip(c0[:, 0], 0, GRID_W - COL_BAND)[:, None] + np.arange(COL_BAND)
    col_valid = (band[:, None, :] >= c0[:, :, None]) & (band[:, None, :] < c0[:, :, None] + WIN_C)
    col_idx = np.clip(band[:, None, :] - qcol[:, :, None] + WIN_C - 1, 0, 2 * WIN_C - 2)
    valid = jnp.asarray(col_valid)[:, :, None, :]
    rpb_c = rpb.astype(jnp.float32)[:, :, col_idx]

    def one_row(r):
        r0 = jnp.clip(r - wr // 2, 0, rows - wr)
        kr = lax.dynamic_slice_in_dim(k, r0, wr, axis=2)[:, :, :, band]
        vr = lax.dynamic_slice_in_dim(v, r0, wr, axis=2)[:, :, :, band]
        qr = lax.dynamic_index_in_dim(q, r, axis=2, keepdims=False)
        qr = qr.reshape(bsz, NA_HEADS, n_cb, COL_BLOCK, NA_DH)
        row_idx = r0 + jnp.arange(wr) - r + WIN_R - 1
        bias = jnp.take(rpb_c, row_idx, axis=1).transpose(0, 2, 3, 1, 4)
        s = jnp.einsum('bhnqd,bhrnkd->bhnqrk', qr, kr).astype(jnp.float32) + bias
        s = jnp.where(valid, s, NEG_BIG).reshape(bsz, NA_HEADS, n_cb, COL_BLOCK, wr * COL_BAND)
        p = jax.nn.softmax(s, axis=-1).reshape(bsz, NA_HEADS, n_cb, COL_BLOCK, wr, COL_BAND)
        o = jnp.einsum('bhnqrk,bhrnkd->bhnqd', p.astype(v.dtype), vr)
        return o.reshape(bsz, NA_HEADS, GRID_W, NA_DH)

    o = lax.map(one_row, jnp.arange(rows))
    o = o.transpose(1, 0, 3, 2, 4).reshape(bsz, seq, D_MODEL)
    return o @ w_out


def _layer_norm(t, g, b):
    t32 = t.astype(jnp.float32)
    mu = jnp.mean(t32, -1, keepdims=True)
    var = jnp.mean(jnp.square(t32 - mu), -1, keepdims=True)
    return ((t32 - mu) * lax.rsqrt(var + LN_EPS) * g + b).astype(t.dtype)


def _swiglu(t, wg, wu, wd):
    return (jax.nn.silu(t @ wg) * (t @ wu)) @ wd


def setup_inputs(seed: int = 0) -> dict:
    key = jax.random.key(seed)
    ks = jax.random.split(key, 20)
    f32 = jnp.float32

    def nrm(k, shape, scale):
        return jax.random.normal(k, shape, f32) * scale

    x = nrm(ks[0], (BATCH, SEQ, D_MODEL), 1.0)
    w_in_even = nrm(ks[1], (N_EVEN, D_MODEL, IN_COLS), D_MODEL ** -0.5)
    f_bias = jnp.tile(jnp.linspace(3.0, 6.0, B_HEADS, dtype=f32), 2)
    gate_bias_even = jnp.concatenate(
        [nrm(ks[2], (N_EVEN, 2 * B_HEADS), 0.1),
         f_bias + nrm(ks[3], (N_EVEN, 2 * B_HEADS), 0.1)], axis=-1)
    lb_raw = nrm(ks[4], (2, N_EVEN, A_KEY), 0.5)
    conv_qk = nrm(ks[5], (N_EVEN, CONV_W, 2 * B_WIDTH), CONV_W ** -0.5)
    gn_hgrn = 1.0 + nrm(ks[6], (N_EVEN, A_WIDTH), 0.02)
    gn_mlstm = 1.0 + nrm(ks[7], (N_EVEN, B_WIDTH), 0.02)
    w_out_even = nrm(ks[8], (N_EVEN, MIX_WIDTH, D_MODEL), BETA * MIX_WIDTH ** -0.5)
    w_qkv_odd = nrm(ks[9], (N_ODD, D_MODEL, 3 * D_MODEL), D_MODEL ** -0.5)
    rpb_odd = nrm(ks[10], (N_ODD, NA_HEADS, 2 * WIN_R - 1, 2 * WIN_C - 1), 0.05)
    w_out_odd = nrm(ks[11], (N_ODD, D_MODEL, D_MODEL), BETA * D_MODEL ** -0.5)
    ln_mix_g = 1.0 + nrm(ks[12], (DEPTH, D_MODEL), 0.02)
    ln_mix_b = nrm(ks[13], (DEPTH, D_MODEL), 0.02)
    ln_ffn_g = 1.0 + nrm(ks[14], (DEPTH, D_MODEL), 0.02)
    ln_ffn_b = nrm(ks[15], (DEPTH, D_MODEL), 0.02)
    w_ffn_gate = nrm(ks[16], (DEPTH, D_MODEL, FFN_HIDDEN), D_MODEL ** -0.5)
    w_ffn_up = nrm(ks[17], (DEPTH, D_MODEL, FFN_HIDDEN), D_MODEL ** -0.5)
    w_ffn_down = nrm(ks[18], (DEPTH, FFN_HIDDEN, D_MODEL), BETA * FFN_HIDDEN ** -0.5)
    return {"x": x, "w_in_even": w_in_even, "gate_bias_even": gate_bias_even,
            "lb_raw": lb_raw, "conv_qk": conv_qk, "gn_hgrn": gn_hgrn, "gn_mlstm": gn_mlstm,
            "w_out_even": w_out_even, "w_qkv_odd": w_qkv_odd, "rpb_odd": rpb_odd,
            "w_out_odd": w_out_odd, "ln_mix_g": ln_mix_g, "ln_mix_b": ln_mix_b,
            "ln_ffn_g": ln_ffn_g, "ln_ffn_b": ln_ffn_b, "w_ffn_gate": w_ffn_gate,
            "w_ffn_up": w_ffn_up, "w_ffn_down": w_ffn_down}


def reference(x, w_in_even, gate_bias_even, lb_raw, conv_qk, gn_hgrn, gn_mlstm,
              w_out_even, w_qkv_odd, rpb_odd, w_out_odd, ln_mix_g, ln_mix_b,
              ln_ffn_g, ln_ffn_b, w_ffn_gate, w_ffn_up, w_ffn_down):
    soft = jax.nn.softmax(lb_raw.astype(jnp.float32), axis=1)
    lower_bounds = jnp.cumsum(soft, axis=1) - soft[:, :1]
    h = x
    for layer in range(DEPTH):
        j = layer // 2
        if layer % 2 == 0:
            mix = _even_mixer(h, w_in_even[j], gate_bias_even[j], lower_bounds[0, j],
                              lower_bounds[1, j], conv_qk[j], gn_hgrn[j], gn_mlstm[j],
                              w_out_even[j])
        else:
            mix = _neighbourhood_attention(h, w_qkv_odd[j], rpb_odd[j], w_out_odd[j])
        h = _layer_norm(ALPHA * h + mix, ln_mix_g[layer], ln_mix_b[layer])
        h = _layer_norm(ALPHA * h + _swiglu(h, w_ffn_gate[layer], w_ffn_up[layer], w_ffn_down[layer]),
                        ln_ffn_g[layer], ln_ffn_b[layer])
    return h
```

```python
import numpy as np
from contextlib import ExitStack
import concourse.bass as bass
import concourse.mybir as mybir
from concourse.bass_utils import run_bass_kernel_spmd

F32 = mybir.dt.float32
BF16 = mybir.dt.bfloat16
AF = mybir.ActivationFunctionType
ALU = mybir.AluOpType
AX = mybir.AxisListType

D = 1024
FF = 2816
ALPHA = (2.0 * 4) ** 0.25
LN_EPS = 1e-5
GN_EPS = 1e-6


class Trk:
    __slots__ = ("w", "r", "dsem", "dcnt", "name")

    def __init__(self, name=""):
        self.w = None
        self.r = {}
        self.dsem = None
        self.dcnt = 0
        self.name = name


class Tile:
    def __init__(self, h, t):
        self.h = h
        self.t = t

    def __getitem__(self, k):
        return self.h[k]


class KB:
    def __init__(self, nc, stack):
        self.nc = nc
        self.stack = stack
        self.eng = {"pe": nc.tensor, "act": nc.scalar, "dve": nc.vector,
                    "pool": nc.gpsimd, "sp": nc.sync}
        self.csem = {}
        self.cnt = {}
        for e in self.eng:
            self.csem[e] = stack.enter_context(nc.semaphore("cs_" + e))
            self.cnt[e] = 0
        self.waited = {}
        self.alltrk = []
        self.free_dsems = []
        self.nsem = 0
        self.uid = 0
        self.n_ins = 0
        self._semcnt = {}
        self.ps = stack.enter_context(nc.psum_tensor("psum_all", [128, 8, 512], F32))
        self.bank = [self.trk("bank%d" % i) for i in range(8)]

    def trk(self, name=""):
        t = Trk(name)
        self.alltrk.append(t)
        return t

    def tile(self, st, name, shape, dtype):
        self.uid += 1
        h = st.enter_context(self.nc.sbuf_tensor("%s_%d" % (name, self.uid), list(shape), dtype))
        return Tile(h, self.trk(name))

    def _dsem(self, t):
        if t.dsem is None:
            if self.free_dsems:
                t.dsem = self.free_dsems.pop()
            else:
                self.nsem += 1
                t.dsem = self.stack.enter_context(self.nc.semaphore("ds_%d" % self.nsem))
            t.dcnt = self._semcnt.get(id(t.dsem), 0)
        return t.dsem

    def release(self, tiles):
        for x in tiles:
            t = x.t if isinstance(x, Tile) else x
            if t.dsem is not None:
                self._semcnt[id(t.dsem)] = t.dcnt
                self.free_dsems.append(t.dsem)
                t.dsem = None
            if t in self.alltrk:
                self.alltrk.remove(t)

    def _wait(self, e, tok):
        sem, val = tok
        key = (e, id(sem))
        if self.waited.get(key, 0) >= val:
            return
        self.eng[e].wait_ge(sem, val)
        self.n_ins += 1
        self.waited[key] = val

    def _deps(self, e, reads, writes):
        own = self.csem[e]
        for t in reads:
            if t.w is not None:
                if not (e == "pe" and t.w[0] is own):
                    self._wait(e, t.w)
        for t in writes:
            if t.w is not None:
                if not (e == "pe" and t.w[0] is own):
                    self._wait(e, t.w)
            for tok in t.r.values():
                if not (e == "pe" and tok[0] is own):
                    self._wait(e, tok)

    @staticmethod
    def _trks(lst):
        return [x.t if isinstance(x, Tile) else x for x in lst]

    def op(self, e, fn, reads=(), writes=()):
        reads = self._trks(reads)
        writes = self._trks(writes)
        self._deps(e, reads, writes)
        ins = fn(self.eng[e])
        self.cnt[e] += 1
        self.n_ins += 1
        ins.then_inc(self.csem[e], 1)
        tok = (self.csem[e], self.cnt[e])
        for t in reads:
            t.r[id(tok[0])] = tok
        for t in writes:
            t.w = tok
            t.r = {}
        return ins

    def dma(self, q, out, in_, owner, reads=(), writes=(), group=False):
        reads = self._trks(reads)
        writes = self._trks(writes)
        owner = owner.t if isinstance(owner, Tile) else owner
        self._deps(q, reads, writes)
        sem = self._dsem(owner)
        if owner.dcnt > 0 and not group:
            self._wait(q, (sem, owner.dcnt))
        ins = self.eng[q].dma_start(out=out, in_=in_)
        self.n_ins += 1
        owner.dcnt += 16
        ins.then_inc(sem, 16)
        tok = (sem, owner.dcnt)
        for t in reads:
            t.r[id(sem)] = tok
        for t in writes:
            t.w = tok
            t.r = {}
        return ins

    def barrier(self):
        toks = [(self.csem[e], self.cnt[e]) for e in self.eng if self.cnt[e] > 0]
        for t in self.alltrk:
            if t.dsem is not None and t.dcnt > 0:
                toks.append((t.dsem, t.dcnt))
        for e in self.eng:
            for tok in toks:
                if tok[0] is self.csem[e]:
                    continue
                self._wait(e, tok)

    def final_wait(self, q="sp"):
        toks = [(self.csem[e], self.cnt[e]) for e in self.eng if self.cnt[e] > 0]
        for t in self.alltrk:
            if t.dsem is not None and t.dcnt > 0:
                toks.append((t.dsem, t.dcnt))
        for tok in toks:
            self._wait(q, tok)


def bc_load(kb, st, name, vec_ap, n):
    t = kb.tile(st, name, [128, n], F32)
    kb.dma("sp", t[:], vec_ap.partition_broadcast(128), owner=t, writes=[t])
    return t


def load_weight_bf16(kb, W, w_ap, kchunks):
    src = w_ap.rearrange("(c p) n -> p c n", p=128)
    for c in range(kchunks):
        kb.dma("pool", W[:, c, :], src[:, c, :], owner=W, writes=[W] if c == 0 else [], group=(c > 0))
    W.t.w = (W.t.dsem, W.t.dcnt)


def ln_epilogue(kb, res_ap, res_trk, y_ap, y_banks, sm, z, gbc, bbc, out_tile):
    kb.op("dve", lambda e: e.scalar_tensor_tensor(out=z[:], in0=res_ap, scalar=ALPHA, in1=y_ap,
                                                  op0=ALU.mult, op1=ALU.add),
          reads=[res_trk] + y_banks, writes=[z])
    kb.op("act", lambda e: e.activation(out=out_tile[:], in_=z[:], func=AF.Identity, accum_out=sm[:, 0:1]),
          reads=[z], writes=[out_tile, sm])
    kb.op("act", lambda e: e.activation(out=out_tile[:], in_=z[:], func=AF.Square, accum_out=sm[:, 1:2]),
          reads=[z], writes=[out_tile, sm])
    kb.op("dve", lambda e: e.tensor_scalar(out=sm[:, 2:4], in0=sm[:, 0:2], scalar1=1.0 / D, scalar2=None,
                                           op0=ALU.mult), reads=[sm], writes=[sm])
    kb.op("dve", lambda e: e.tensor_tensor(out=sm[:, 4:5], in0=sm[:, 2:3], in1=sm[:, 2:3], op=ALU.mult),
          reads=[sm], writes=[sm])
    kb.op("dve", lambda e: e.tensor_tensor(out=sm[:, 5:6], in0=sm[:, 3:4], in1=sm[:, 4:5], op=ALU.subtract),
          reads=[sm], writes=[sm])
    kb.op("dve", lambda e: e.tensor_scalar(out=sm[:, 5:6], in0=sm[:, 5:6], scalar1=LN_EPS, scalar2=None,
                                           op0=ALU.add), reads=[sm], writes=[sm])
    kb.op("act", lambda e: e.activation(out=sm[:, 6:7], in_=sm[:, 5:6], func=AF.Sqrt), reads=[sm], writes=[sm])
    kb.op("dve", lambda e: e.reciprocal(out=sm[:, 6:7], in_=sm[:, 6:7]), reads=[sm], writes=[sm])
    kb.op("dve", lambda e: e.scalar_tensor_tensor(out=sm[:, 7:8], in0=sm[:, 2:3], scalar=-1.0, in1=sm[:, 6:7],
                                                  op0=ALU.mult, op1=ALU.mult), reads=[sm], writes=[sm])
    kb.op("act", lambda e: e.activation(out=z[:], in_=z[:], func=AF.Identity, scale=sm[:, 6:7], bias=sm[:, 7:8]),
          reads=[sm, z], writes=[z])
    kb.op("dve", lambda e: e.tensor_tensor(out=out_tile[:], in0=z[:], in1=gbc[:], op=ALU.mult),
          reads=[z, gbc], writes=[out_tile])
    kb.op("dve", lambda e: e.tensor_tensor(out=out_tile[:], in0=out_tile[:], in1=bbc[:], op=ALU.add),
          reads=[bbc, out_tile], writes=[out_tile])


def const_load(kb, st, name, ap, shape, dtype=F32):
    t = kb.tile(st, name, shape, dtype)
    kb.dma("pool" if dtype != F32 else "sp", t[:], ap, owner=t, writes=[t])
    return t


def do_transposes(kb, ident, src_fn, src_trk, nsub, xT, tbanks, alt=[0]):
    for s in range(nsub):
        for g in range(2):
            b = tbanks[alt[0] % len(tbanks)]
            alt[0] += 1
            for j in range(4):
                kc = g * 4 + j
                kb.op("pe", lambda e, kc=kc, j=j: e.transpose(out=kb.ps[:, b, j * 128:(j + 1) * 128],
                                                             in_=src_fn(s, kc), identity=ident[:]),
                      reads=[src_trk, ident], writes=[kb.bank[b]])
            dst = xT[:, g * 4:(g + 1) * 4, s * 128:(s + 1) * 128]
            srcp = kb.ps[:, b, :].rearrange("p (j t) -> p j t", j=4)
            eng = "act" if (alt[0] % 2 == 0) else "dve"
            if eng == "act":
                kb.op("act", lambda e: e.copy(out=dst, in_=srcp), reads=[kb.bank[b]], writes=[xT])
            else:
                kb.op("dve", lambda e: e.tensor_copy(out=dst, in_=srcp), reads=[kb.bank[b]], writes=[xT])


def phase_ffn(kb, NT, h_dram, wg, wu, wd, g_ap, b_ap, ident_ap, out_dram=None):
    if out_dram is None:
        out_dram = h_dram
    TT = 256
    NM = FF // 128
    with ExitStack() as st:
        Wg = kb.tile(st, "Wg", [128, 8, FF], BF16)
        Wu = kb.tile(st, "Wu", [128, 8, FF], BF16)
        Wd = kb.tile(st, "Wd", [128, NM, D], BF16)
        ident = const_load(kb, st, "ident", ident_ap, [128, 128])
        gbc = bc_load(kb, st, "gbc", g_ap, D)
        bbc = bc_load(kb, st, "bbc", b_ap, D)
        load_weight_bf16(kb, Wg, wg, 8)
        load_weight_bf16(kb, Wu, wu, 8)
        load_weight_bf16(kb, Wd, wd, NM)
        hin = [kb.tile(st, "hin", [128, 2, D], F32) for _ in range(2)]
        xT = [kb.tile(st, "xT", [128, 8, TT], BF16) for _ in range(2)]
        hT = kb.tile(st, "hT", [128, NM, TT], BF16)
        hTt = [kb.trk("hT%d" % m) for m in range(NM)]
        sg = [kb.tile(st, "sg", [128, TT], F32) for _ in range(2)]
        z = [kb.tile(st, "z", [128, D], F32) for _ in range(2)]
        ot = [kb.tile(st, "ot", [128, D], F32) for _ in range(2)]
        sm = [kb.tile(st, "sm", [128, 8], F32) for _ in range(2)]
        ntile = NT // TT
        gu = 0
        def load_h(ti):
            rows = h_dram[ti * TT:(ti + 1) * TT, :].rearrange("(s p) f -> p s f", p=128)
            kb.dma("sp", hin[ti % 2][:], rows, owner=hin[ti % 2], writes=[hin[ti % 2]])
        load_h(0)
        for ti in range(ntile):
            hb = hin[ti % 2]
            xb = xT[ti % 2]
            if ti + 1 < ntile:
                load_h(ti + 1)
            do_transposes(kb, ident, lambda s, kc: hb[:, s, kc * 128:(kc + 1) * 128], hb, 2, xb, [0, 1])
            for m in range(NM):
                b = 2 + (gu % 2)
                gu += 1
                for kc in range(8):
                    kb.op("pe", lambda e, kc=kc: e.matmul(kb.ps[:, b, 0:TT], lhsT=Wg[:, kc, m * 128:(m + 1) * 128],
                                                          rhs=xb[:, kc, :], start=(kc == 0), stop=(kc == 7)),
                          reads=[Wg, xb], writes=[kb.bank[b]])
                for kc in range(8):
                    kb.op("pe", lambda e, kc=kc: e.matmul(kb.ps[:, b, TT:2 * TT], lhsT=Wu[:, kc, m * 128:(m + 1) * 128],
                                                          rhs=xb[:, kc, :], start=(kc == 0), stop=(kc == 7)),
                          reads=[Wu, xb], writes=[kb.bank[b]])
                sgt = sg[m % 2]
                kb.op("act", lambda e: e.activation(out=sgt[:], in_=kb.ps[:, b, 0:TT], func=AF.Silu),
                      reads=[kb.bank[b]], writes=[sgt])
                kb.op("dve", lambda e: e.tensor_tensor(out=hT[:, m, :], in0=sgt[:], in1=kb.ps[:, b, TT:2 * TT],
                                                       op=ALU.mult),
                      reads=[sgt, kb.bank[b]], writes=[hTt[m]])
            for s in range(2):
                yb = [4 + 2 * s, 5 + 2 * s]
                for m in range(NM):
                    for nh in range(2):
                        kb.op("pe", lambda e, nh=nh: e.matmul(kb.ps[:, yb[nh], :], lhsT=hT[:, m, s * 128:(s + 1) * 128],
                                                              rhs=Wd[:, m, nh * 512:(nh + 1) * 512],
                                                              start=(m == 0), stop=(m == NM - 1)),
                              reads=[hTt[m], Wd], writes=[kb.bank[yb[nh]]])
                y_ap = kb.ps[:, yb[0]:yb[0] + 2, :].rearrange("p b n -> p (b n)")
                k2 = (ti * 2 + s) % 2
                ln_epilogue(kb, hb[:, s, :], hb.t, y_ap, [kb.bank[yb[0]], kb.bank[yb[1]]], sm[k2], z[k2], gbc, bbc,
                            ot[k2])
                r0 = ti * TT + s * 128
                kb.dma("pool", out_dram[r0:r0 + 128, :], ot[k2][:], owner=ot[k2], reads=[ot[k2]])
        kb.barrier()
        kb.release([Wg, Wu, Wd, ident, gbc, bbc] + hin + xT + [hT] + hTt + sg + z + ot + sm)


def phase_proj(kb, NT, h_dram, w_ap, C, outs, ident_ap):
    TT = 512
    with ExitStack() as st:
        W = kb.tile(st, "W", [128, 8, C], BF16)
        ident = const_load(kb, st, "ident", ident_ap, [128, 128])
        load_weight_bf16(kb, W, w_ap, 8)
        hin = [kb.tile(st, "hin", [128, 4, D], F32) for _ in range(2)]
        xT = [kb.tile(st, "xT", [128, 8, TT], BF16) for _ in range(2)]
        stg32 = [kb.tile(st, "stg32", [128, 512], F32) for _ in range(4)]
        stg16 = [kb.tile(st, "stg16", [128, 512], BF16) for _ in range(4)]
        cnt = {"s32": 0, "s16": 0, "bank": 0, "ev": 0}
        ntile = NT // TT

        def load_h(ti):
            rows = h_dram[ti * TT:(ti + 1) * TT, :].rearrange("(s p) f -> p s f", p=128)
            kb.dma("sp", hin[ti % 2][:], rows, owner=hin[ti % 2], writes=[hin[ti % 2]])

        def evac(b, n, dtype, scale, np_=128):
            if dtype == F32:
                stg = stg32[cnt["s32"] % 4]
                cnt["s32"] += 1
            else:
                stg = stg16[cnt["s16"] % 4]
                cnt["s16"] += 1
            cnt["ev"] += 1
            if cnt["ev"] % 2 == 0:
                kb.op("act", lambda e: e.activation(out=stg[0:np_, 0:n], in_=kb.ps[0:np_, b, 0:n], func=AF.Identity,
                                                    scale=float(scale)), reads=[kb.bank[b]], writes=[stg])
            else:
                kb.op("dve", lambda e: e.tensor_scalar(out=stg[0:np_, 0:n], in0=kb.ps[0:np_, b, 0:n], scalar1=float(scale),
                                                       scalar2=None, op0=ALU.mult), reads=[kb.bank[b]], writes=[stg])
            return stg

        load_h(0)
        for ti in range(ntile):
            hb = hin[ti % 2]
            xb = xT[ti % 2]
            if ti + 1 < ntile:
                load_h(ti + 1)
            do_transposes(kb, ident, lambda s, kc: hb[:, s, kc * 128:(kc + 1) * 128], hb, 4, xb, [0, 1])
            t0 = ti * TT
            for o in outs:
                sc = o.get("scale", 1.0)
                if o["kind"] == "F":
                    fw = o.get("fw", 128)
                    for c in range(-(-o["n"] // fw)):
                        col = o["c0"] + c * fw
                        wd_ = min(fw, o["n"] - c * fw)
                        b = 2 + cnt["bank"] % 6
                        cnt["bank"] += 1
                        for kc in range(8):
                            kb.op("pe", lambda e, kc=kc: e.matmul(kb.ps[0:wd_, b, :], lhsT=W[:, kc, col:col + wd_],
                                                                  rhs=xb[:, kc, :], start=(kc == 0), stop=(kc == 7)),
                                  reads=[W, xb], writes=[kb.bank[b]])
                        stg = evac(b, 512, o["dtype"], sc, wd_)
                        kb.dma("pool", o["dram"][c, 0:wd_, t0:t0 + TT], stg[0:wd_, :], owner=stg, reads=[stg])
                else:
                    n_all = o["n"]
                    for cb in range(0, n_all, 512):
                        n = min(512, n_all - cb)
                        col = o["c0"] + cb
                        for s in range(4):
                            b = 2 + cnt["bank"] % 6
                            cnt["bank"] += 1
                            for kc in range(8):
                                kb.op("pe", lambda e, kc=kc: e.matmul(kb.ps[:, b, 0:n],
                                                                      lhsT=xb[:, kc, s * 128:(s + 1) * 128],
                                                                      rhs=W[:, kc, col:col + n],
                                                                      start=(kc == 0), stop=(kc == 7)),
                                      reads=[W, xb], writes=[kb.bank[b]])
                            stg = evac(b, n, o["dtype"], sc)
                            r0 = t0 + s * 128
                            kb.dma("pool", o["dram"][r0:r0 + 128, cb:cb + n], stg[:, 0:n], owner=stg, reads=[stg])
        kb.barrier()
        kb.release([W, ident] + hin + xT + stg32 + stg16)


def phase_m3(kb, NT, mix_dram, h_dram, w_ap, g_ap, b_ap, ident_ap, hout_dram=None):
    if hout_dram is None:
        hout_dram = h_dram
    with ExitStack() as st:
        W = kb.tile(st, "Wo", [128, 8, D], BF16)
        identb = const_load(kb, st, "identb", ident_ap, [128, 128], BF16)
        gbc = bc_load(kb, st, "gbc", g_ap, D)
        bbc = bc_load(kb, st, "bbc", b_ap, D)
        load_weight_bf16(kb, W, w_ap, 8)
        mx = [kb.tile(st, "mx", [128, D], BF16) for _ in range(2)]
        hin = [kb.tile(st, "hin", [128, D], F32) for _ in range(2)]
        oT = [kb.tile(st, "oT", [128, 8, 128], BF16) for _ in range(2)]
        z = [kb.tile(st, "z", [128, D], F32) for _ in range(2)]
        ot = [kb.tile(st, "ot", [128, D], F32) for _ in range(2)]
        sm = [kb.tile(st, "sm", [128, 8], F32) for _ in range(2)]
        ntile = NT // 128

        def load(ti):
            k = ti % 2
            kb.dma("sp", mx[k][:], mix_dram[ti * 128:(ti + 1) * 128, :], owner=mx[k], writes=[mx[k]])
            kb.dma("sp", hin[k][:], h_dram[ti * 128:(ti + 1) * 128, :], owner=hin[k], writes=[hin[k]])

        load(0)
        for ti in range(ntile):
            k = ti % 2
            if ti + 1 < ntile:
                load(ti + 1)
            tb = ti % 2
            psb = kb.ps[:, tb, :].bitcast(BF16)
            for kc in range(8):
                kb.op("pe", lambda e, kc=kc: e.transpose(out=psb[:, kc * 128:(kc + 1) * 128],
                                                         in_=mx[k][:, kc * 128:(kc + 1) * 128], identity=identb[:]),
                      reads=[mx[k], identb], writes=[kb.bank[tb]])
            kb.op("act", lambda e: e.copy(out=oT[k][:].rearrange("p c t -> p (c t)"), in_=psb),
                  reads=[kb.bank[tb]], writes=[oT[k]])
            yb = [2 + 2 * k, 3 + 2 * k]
            for kc in range(8):
                for nh in range(2):
                    kb.op("pe", lambda e, kc=kc, nh=nh: e.matmul(kb.ps[:, yb[nh], :], lhsT=oT[k][:, kc, :],
                                                                 rhs=W[:, kc, nh * 512:(nh + 1) * 512],
                                                                 start=(kc == 0), stop=(kc == 7)),
                          reads=[oT[k], W], writes=[kb.bank[yb[nh]]])
            y_ap = kb.ps[:, yb[0]:yb[0] + 2, :].rearrange("p b n -> p (b n)")
            ln_epilogue(kb, hin[k][:], hin[k].t, y_ap, [kb.bank[yb[0]], kb.bank[yb[1]]], sm[k], z[k], gbc, bbc, ot[k])
            kb.dma("pool", hout_dram[ti * 128:(ti + 1) * 128, :], ot[k][:], owner=ot[k], reads=[ot[k]])
        kb.barrier()
        kb.release([W, identb, gbc, bbc] + mx + hin + oT + z + ot + sm)


def na_host_tables(rpb):
    kcol = np.arange(64)[:, None]
    qcol = np.arange(64)[None, :]
    idx = np.clip(kcol - qcol + 15, 0, 30)
    G = np.empty((2, 64, 32, 14, 64), np.float32)
    for half in range(2):
        for slot in range(14):
            G[half, :, :, slot, :] = rpb[:, slot + half, :][:, idx].transpose(1, 0, 2)
    return np.ascontiguousarray(G.reshape(128, 32 * 14 * 64))


def na_const_mask():
    kcol = np.arange(64)[:, None]
    qcol = np.arange(64)[None, :]
    c0 = np.clip(qcol - 8, 0, 48)
    valid = (kcol >= c0) & (kcol < c0 + 16)
    m = np.where(valid, 0.0, -30000.0).astype(np.float32)
    return np.ascontiguousarray(np.concatenate([m, m], 0))


def phase_na(kb, NSEQ, S, qT_d, kT_d, v_d, G_d, negm_d, mix_d, dbg=9):
    R = S // 64
    NP = R // 2
    with ExitStack() as st:
        TT2 = kb.tile(st, "TT2", [128, 32 * 14, 64], BF16)
        negm = const_load(kb, st, "negm", negm_d, [128, 64])
        gst = [kb.tile(st, "gst", [128, 32, 64], F32) for _ in range(2)]
        for ch in range(14):
            g = gst[ch % 2]
            kb.dma("sp", g[:].rearrange("p a b -> p (a b)"), G_d[:, ch * 2048:(ch + 1) * 2048], owner=g, writes=[g])
            kb.op("dve", lambda e: e.tensor_tensor(out=g[:], in0=g[:], in1=negm[:].unsqueeze(1).to_broadcast([128, 32, 64]),
                                                   op=ALU.add), reads=[g, negm], writes=[g])
            kb.op("act", lambda e: e.activation(out=TT2[:, ch * 32:(ch + 1) * 32, :], in_=g[:], func=AF.Exp),
                  reads=[g], writes=[TT2])
        qT = kb.tile(st, "qT", [128, 2, S], BF16)
        kT = kb.tile(st, "kT", [128, 2, S], BF16)
        vst = kb.tile(st, "vst", [128, NP, 192], BF16)
        VE = kb.tile(st, "VE", [128, NP, 6, 34], BF16)
        VO = kb.tile(st, "VO", [128, NP, 6, 34], BF16)
        kb.op("dve", lambda e: e.memset(VE[:], 1.0), writes=[VE])
        kb.op("dve", lambda e: e.memset(VO[:], 1.0), writes=[VO])
        Et = [kb.tile(st, "Et", [128, 3, 128], BF16) for _ in range(3)]
        Pt = [kb.tile(st, "Pt", [128, 3, 2, 64], BF16) for _ in range(3)]
        rd = [kb.tile(st, "rd", [64, 6, 1], F32) for _ in range(2)]
        orow = [kb.tile(st, "orow", [64, 6, 32], BF16) for _ in range(4)]
        it = 0
        for sq in range(NSEQ if dbg >= 2 else 0):
            t0 = sq * S
            import os
            for hg in range(int(os.environ.get('NA_NG', '6'))):
                h0 = hg * 6
                NH = min(6, 32 - h0)
                NCH = -(-NH // 3)
                for c in range(NCH):
                    kb.dma("sp", qT[0:96, c, :], qT_d[hg * 2 + c, 0:96, t0:t0 + S], owner=qT,
                           writes=[qT] if c == 0 else [], group=(c > 0))
                qT.t.w = (qT.t.dsem, qT.t.dcnt)
                for c in range(NCH):
                    kb.dma("sp", kT[0:96, c, :], kT_d[hg * 2 + c, 0:96, t0:t0 + S], owner=kT,
                           writes=[kT] if c == 0 else [], group=(c > 0))
                kT.t.w = (kT.t.dsem, kT.t.dcnt)
                vc0, vc1 = h0 * 32, (h0 + NH) * 32
                kb.dma("sp", vst[:, :, 0:NH * 32], v_d[t0:t0 + S, vc0:vc1].rearrange("(j p) f -> p j f", p=128),
                       owner=vst, writes=[vst])
                kb.op("dve", lambda e: e.tensor_copy(out=VE[:, :, 0:NH, 0:32],
                                                     in_=vst[:, :, 0:NH * 32].rearrange("p j (h d) -> p j h d", h=NH)),
                      reads=[vst], writes=[VE])
                kb.dma("sp", vst[:, 0:NP - 1, 0:NH * 32],
                       v_d[t0 + 64:t0 + S - 64, vc0:vc1].rearrange("(j p) f -> p j f", p=128),
                       owner=vst, writes=[vst])
                kb.op("dve", lambda e: e.tensor_copy(out=VO[:, 0:NP - 1, 0:NH, 0:32],
                                                     in_=vst[:, 0:NP - 1, 0:NH * 32].rearrange("p j (h d) -> p j h d", h=NH)),
                      reads=[vst], writes=[VO])
                NI = min(3, NH)
                for r in range(R if dbg >= 3 else 0):
                    r0 = min(max(r - 4, 0), R - 8)
                    bo = 6 + (r % 2)
                    for m in range(4):
                        rho = r0 + 2 * m
                        slot = rho - r + 7
                        b0 = 3 * (it % 2)
                        e3 = it % 3
                        it += 1
                        for c in range(NCH):
                            for i in range(NI):
                                kb.op("pe", lambda e: e.matmul(kb.ps[:, b0 + i, c * 64:(c + 1) * 64],
                                                               lhsT=kT[32 * i:32 * i + 32, c, rho * 64:rho * 64 + 128],
                                                               rhs=qT[32 * i:32 * i + 32, c, r * 64:r * 64 + 64],
                                                               start=True, stop=True),
                                      reads=[kT, qT], writes=[kb.bank[b0 + i]])
                        if dbg < 4:
                            continue
                        sbanks = [kb.bank[b0 + i] for i in range(NI)]
                        kb.op("act", lambda e: e.activation(out=Et[e3][:, 0:NI, 0:NCH * 64],
                                                            in_=kb.ps[:, b0:b0 + NI, 0:NCH * 64], func=AF.Exp),
                              reads=sbanks, writes=[Et[e3]])
                        tsl = TT2[:].rearrange("p (h s) q -> p h s q", s=14)[:, h0:h0 + NH, slot, :]
                        tsl = tsl.rearrange("p (c i) q -> p i c q", i=NI)
                        kb.op("dve", lambda e: e.tensor_tensor(out=Pt[e3][:, 0:NI, 0:NCH, :],
                                                               in0=Et[e3][:, 0:NI, 0:NCH * 64].rearrange("p i (c q) -> p i c q", c=NCH),
                                                               in1=tsl, op=ALU.mult),
                              reads=[Et[e3], TT2], writes=[Pt[e3]])
                        V = VE if rho % 2 == 0 else VO
                        for hh in range(NH if dbg >= 5 else 0):
                            c, i = hh // 3, hh % 3
                            kb.op("pe", lambda e: e.matmul(kb.ps[0:64, bo, hh * 34:hh * 34 + 33],
                                                           lhsT=Pt[e3][:, i, c, :], rhs=V[:, rho // 2, hh, 0:33],
                                                           start=(m == 0 and hh == 0), stop=(m == 3 and hh == NH - 1), skip_group_check=True),
                                  reads=[Pt[e3], V], writes=[kb.bank[bo]])
                    if dbg < 6:
                        continue
                    k2 = r % 2
                    k4 = r % 4
                    po = kb.ps[0:64, bo, 0:NH * 34].rearrange("p (h d) -> p h d", d=34)
                    kb.op("dve", lambda e: e.reciprocal(out=rd[k2][:, 0:NH, :], in_=po[:, :, 32:33]),
                          reads=[kb.bank[bo]], writes=[rd[k2]])
                    kb.op("dve", lambda e: e.tensor_tensor(out=orow[k4][:, 0:NH, :], in0=po[:, :, 0:32],
                                                           in1=rd[k2][:, 0:NH, :].to_broadcast([64, NH, 32]), op=ALU.mult),
                          reads=[kb.bank[bo], rd[k2]], writes=[orow[k4]])
                    kb.dma("pool", mix_d[t0 + r * 64:t0 + r * 64 + 64, vc0:vc1],
                           orow[k4][:, 0:NH, :].rearrange("p h d -> p (h d)"), owner=orow[k4], reads=[orow[k4]])
        kb.barrier()
        kb.release([TT2, negm, qT, kT, vst, VE, VO] + gst + Et + Pt + rd + orow)


def phase_hgrn2(kb, NSEQ, S, qT_d, zT_d, v_d, g_d, lb_d, gna_d, ofw_d, mix_d, cm_d, identb_d, bwd):
    TB = 512
    NCH = TB // 64
    NB = S // TB
    with ExitStack() as st:
        identb = const_load(kb, st, "identb", identb_d, [128, 128], BF16)
        lbc = const_load(kb, st, "lbc", lb_d, [128, 10])
        cm = const_load(kb, st, "cm", cm_d, [128, 640])
        msk = cm[0:64, 64:128] if bwd else cm[0:64, 0:64]
        scanm = cm[:, 128:640]
        gna = None
        if bwd:
            gna = kb.tile(st, "gna", [64, 512], F32)
            kb.dma("sp", gna[:], gna_d.partition_broadcast(64), owner=gna, writes=[gna])
        H = 4
        zt = [kb.tile(st, "zt", [128, TB], F32) for _ in range(H)]
        qt = [kb.tile(st, "qt", [128, TB], F32) for _ in range(H)]
        e_ = [kb.tile(st, "e_", [128, TB], F32) for _ in range(H)]
        r_ = [kb.tile(st, "r_", [128, TB], F32) for _ in range(H)]
        lf = [kb.tile(st, "lf", [128, TB], F32) for _ in range(H)]
        kk = [kb.tile(st, "kk", [128, TB], F32) for _ in range(H)]
        bb = [kb.tile(st, "bb", [128, TB], F32) for _ in range(H)]
        E1 = [kb.tile(st, "E1", [128, TB], F32) for _ in range(H)]
        E3 = [kb.tile(st, "E3", [128, TB], F32) for _ in range(H)]
        qs = [kb.tile(st, "qs", [128, TB], BF16) for _ in range(H)]
        ks = [kb.tile(st, "ks", [128, TB], BF16) for _ in range(H)]
        qe = [kb.tile(st, "qe", [128, TB], BF16) for _ in range(H)]
        vt = [kb.tile(st, "vt", [64, NCH, 128], BF16) for _ in range(H)]
        Sf = [kb.tile(st, "Sf", [128, 128], F32) for _ in range(H)]
        Sb = [kb.tile(st, "Sb", [128, 128], BF16) for _ in range(H)]
        PT = [kb.tile(st, "PT", [64, 64], BF16) for _ in range(H)]
        ktok = [kb.tile(st, "ktok", [64, 128], BF16) for _ in range(H)]
        ost = [kb.tile(st, "ost", [64, 128], F32) for _ in range(H)]
        if bwd:
            oft = [kb.tile(st, "oft", [64, NCH, 128], F32) for _ in range(H)]
            gt = [kb.tile(st, "gt", [64, NCH, 128], F32) for _ in range(H)]
            sg = [kb.tile(st, "sg", [64, 128], F32) for _ in range(H)]
            sq = [kb.tile(st, "sq", [64, 128], F32) for _ in range(H)]
            sm = [kb.tile(st, "sm", [64, 4], F32) for _ in range(H)]
            ob = [kb.tile(st, "ob", [64, 128], BF16) for _ in range(H)]
        bk = 0
        for sqi in range(NSEQ):
            for h in range(H):
                kb.op("dve", lambda e: e.memset(Sf[h][:], 0.0), writes=[Sf[h]])
                kb.op("pool", lambda e: e.memset(Sb[h][:], 0.0), writes=[Sb[h]])
            for bi in (range(NB - 1, -1, -1) if bwd else range(NB)):
                t0 = sqi * S + bi * TB
                for h in range(H):
                    kb.dma("sp", zt[h][:], zT_d[h, :, t0:t0 + TB], owner=zt[h], writes=[zt[h]])
                    kb.dma("sp", qt[h][:], qT_d[h, :, t0:t0 + TB], owner=qt[h], writes=[qt[h]])
                    kb.dma("sp", vt[h][:], v_d[t0:t0 + TB, h * 128:(h + 1) * 128].rearrange("(c p) f -> p c f", p=64),
                           owner=vt[h], writes=[vt[h]])
                    if bwd:
                        kb.dma("sp", oft[h][:], ofw_d[t0:t0 + TB, h * 128:(h + 1) * 128].rearrange("(c p) f -> p c f", p=64),
                               owner=oft[h], writes=[oft[h]])
                        kb.dma("sp", gt[h][:], g_d[t0:t0 + TB, h * 128:(h + 1) * 128].rearrange("(c p) f -> p c f", p=64),
                               owner=gt[h], writes=[gt[h]])
                    lb = lbc[:, 2 * h:2 * h + 1]
                    oml = lbc[:, 2 * h + 1:2 * h + 2]
                    kb.op("act", lambda e: e.activation(out=e_[h][:], in_=zt[h][:], func=AF.Exp, scale=-1.0),
                          reads=[zt[h]], writes=[e_[h]])
                    kb.op("dve", lambda e: e.tensor_scalar(out=r_[h][:], in0=e_[h][:], scalar1=1.0, scalar2=None, op0=ALU.add),
                          reads=[e_[h]], writes=[r_[h]])
                    kb.op("dve", lambda e: e.reciprocal(out=r_[h][:], in_=r_[h][:]), reads=[r_[h]], writes=[r_[h]])
                    kb.op("act", lambda e: e.activation(out=lf[h][:], in_=r_[h][:], func=AF.Ln, scale=oml, bias=lb),
                          reads=[r_[h], lbc], writes=[lf[h]])
                    kb.op("dve", lambda e: e.scalar_tensor_tensor(out=kk[h][:], in0=e_[h][:], scalar=oml, in1=r_[h][:],
                                                                  op0=ALU.mult, op1=ALU.mult),
                          reads=[e_[h], r_[h], lbc], writes=[kk[h]])
                    kb.op("dve", lambda e: e.tensor_tensor_scan(out=bb[h][:], data0=scanm, data1=lf[h][:], initial=0.0,
                                                                op0=ALU.mult, op1=ALU.add),
                          reads=[lf[h], cm], writes=[bb[h]])
                    b3 = bb[h][:].rearrange("p (c l) -> p c l", l=64)
                    if bwd:
                        kb.op("dve", lambda e: e.tensor_tensor(out=lf[h][:], in0=lf[h][:], in1=bb[h][:], op=ALU.subtract),
                              reads=[lf[h], bb[h]], writes=[lf[h]])
                        kb.op("dve", lambda e: e.tensor_tensor(out=E1[h][:].rearrange("p (c l) -> p c l", l=64),
                                                               in0=lf[h][:].rearrange("p (c l) -> p c l", l=64),
                                                               in1=b3[:, :, 63:64].to_broadcast([128, NCH, 64]), op=ALU.add),
                              reads=[lf[h], bb[h]], writes=[E1[h]])
                        kb.op("dve", lambda e: e.tensor_copy(out=bb[h][:], in_=E1[h][:]), reads=[E1[h]], writes=[bb[h]])
                        ref = b3[:, :, 0:1]
                        rpos = 0
                    else:
                        ref = b3[:, :, 63:64]
                        rpos = 63
                    kb.op("act", lambda e: e.activation(out=E3[h][:], in_=bb[h][:], func=AF.Exp), reads=[bb[h]], writes=[E3[h]])
                    kb.op("dve", lambda e: e.tensor_tensor(out=E1[h][:].rearrange("p (c l) -> p c l", l=64), in0=b3,
                                                           in1=ref.to_broadcast([128, NCH, 64]), op=ALU.subtract),
                          reads=[bb[h]], writes=[E1[h]])
                    kb.op("act", lambda e: e.activation(out=E1[h][:], in_=E1[h][:], func=AF.Exp), reads=[E1[h]], writes=[E1[h]])
                    kb.op("dve", lambda e: e.tensor_tensor(out=qs[h][:], in0=qt[h][:], in1=E1[h][:], op=ALU.mult),
                          reads=[qt[h], E1[h]], writes=[qs[h]])
                    kb.op("dve", lambda e: e.reciprocal(out=E1[h][:], in_=E1[h][:]), reads=[E1[h]], writes=[E1[h]])
                    kb.op("dve", lambda e: e.tensor_tensor(out=ks[h][:], in0=kk[h][:], in1=E1[h][:], op=ALU.mult),
                          reads=[kk[h], E1[h]], writes=[ks[h]])
                    kb.op("dve", lambda e: e.tensor_tensor(out=qe[h][:], in0=qt[h][:], in1=E3[h][:], op=ALU.mult),
                          reads=[qt[h], E3[h]], writes=[qe[h]])
                for ci in (range(NCH - 1, -1, -1) if bwd else range(NCH)):
                    cs = slice(ci * 64, ci * 64 + 64)
                    for h in range(H):
                        b1 = (bk % 2)
                        b2 = 2 + (bk % 2)
                        b3_ = 4 + (bk % 2)
                        b4 = 6 + (bk % 2)
                        bk += 1
                        kb.op("pe", lambda e: e.matmul(kb.ps[0:64, b1, 0:64], lhsT=ks[h][:, cs], rhs=qs[h][:, cs],
                                                       start=True, stop=True), reads=[ks[h], qs[h]], writes=[kb.bank[b1]])
                        kb.op("dve", lambda e: e.tensor_tensor(out=PT[h][:], in0=kb.ps[0:64, b1, 0:64], in1=msk, op=ALU.mult),
                              reads=[kb.bank[b1], cm], writes=[PT[h]])
                        pst = kb.ps[:, b2, :].bitcast(BF16)
                        kb.op("pe", lambda e: e.transpose(out=pst[0:64, 0:128], in_=ks[h][:, cs], identity=identb[:]),
                              reads=[ks[h], identb], writes=[kb.bank[b2]])
                        kb.op("act", lambda e: e.copy(out=ktok[h][:], in_=pst[0:64, 0:128]), reads=[kb.bank[b2]], writes=[ktok[h]])
                        kb.op("pe", lambda e: e.matmul(kb.ps[0:64, b3_, 0:128], lhsT=PT[h][:], rhs=vt[h][:, ci, :],
                                                       start=True, stop=False), reads=[PT[h], vt[h]], writes=[kb.bank[b3_]])
                        kb.op("pe", lambda e: e.matmul(kb.ps[0:64, b3_, 0:128], lhsT=qe[h][:, cs], rhs=Sb[h][:],
                                                       start=False, stop=True), reads=[qe[h], Sb[h]], writes=[kb.bank[b3_]])
                        kb.op("pe", lambda e: e.matmul(kb.ps[:, b4, 0:128], lhsT=ktok[h][:], rhs=vt[h][:, ci, :],
                                                       start=True, stop=True), reads=[ktok[h], vt[h]], writes=[kb.bank[b4]])
                        dec = E3[h][:, ci * 64 + rpos:ci * 64 + rpos + 1]
                        kb.op("dve", lambda e: e.scalar_tensor_tensor(out=Sf[h][:], in0=Sf[h][:], scalar=dec, in1=kb.ps[:, b4, 0:128],
                                                                      op0=ALU.mult, op1=ALU.add),
                              reads=[Sf[h], E3[h], kb.bank[b4]], writes=[Sf[h]])
                        kb.op("act", lambda e: e.copy(out=Sb[h][:], in_=Sf[h][:]), reads=[Sf[h]], writes=[Sb[h]])
                        r0 = sqi * S + bi * TB + ci * 64
                        if not bwd:
                            kb.op("act", lambda e: e.copy(out=ost[h][:], in_=kb.ps[0:64, b3_, 0:128]),
                                  reads=[kb.bank[b3_]], writes=[ost[h]])
                            kb.dma("pool", ofw_d[r0:r0 + 64, h * 128:(h + 1) * 128], ost[h][:], owner=ost[h], reads=[ost[h]])
                        else:
                            kb.op("dve", lambda e: e.tensor_tensor(out=ost[h][:], in0=kb.ps[0:64, b3_, 0:128], in1=oft[h][:, ci, :],
                                                                   op=ALU.add), reads=[kb.bank[b3_], oft[h]], writes=[ost[h]])
                            kb.op("act", lambda e: e.activation(out=sq[h][:], in_=ost[h][:], func=AF.Square, accum_out=sm[h][:, 0:1]),
                                  reads=[ost[h]], writes=[sq[h], sm[h]])
                            kb.op("act", lambda e: e.activation(out=sm[h][:, 1:2], in_=sm[h][:, 0:1], func=AF.Ln, scale=1.0 / 128,
                                                                bias=lbc[0:64, 8:9]), reads=[sm[h], lbc], writes=[sm[h]])
                            kb.op("act", lambda e: e.activation(out=sm[h][:, 2:3], in_=sm[h][:, 1:2], func=AF.Exp, scale=-0.5),
                                  reads=[sm[h]], writes=[sm[h]])
                            kb.op("act", lambda e: e.activation(out=sg[h][:], in_=gt[h][:, ci, :], func=AF.Exp, scale=-1.0),
                                  reads=[gt[h]], writes=[sg[h]])
                            kb.op("dve", lambda e: e.tensor_scalar(out=sg[h][:], in0=sg[h][:], scalar1=1.0, scalar2=None, op0=ALU.add),
                                  reads=[sg[h]], writes=[sg[h]])
                            kb.op("dve", lambda e: e.reciprocal(out=sg[h][:], in_=sg[h][:]), reads=[sg[h]], writes=[sg[h]])
                            kb.op("dve", lambda e: e.scalar_tensor_tensor(out=ost[h][:], in0=ost[h][:], scalar=sm[h][:, 2:3],
                                                                          in1=gna[:, h * 128:(h + 1) * 128], op0=ALU.mult, op1=ALU.mult),
                                  reads=[ost[h], sm[h], gna], writes=[ost[h]])
                            kb.op("dve", lambda e: e.tensor_tensor(out=ost[h][:], in0=ost[h][:], in1=gt[h][:, ci, :], op=ALU.mult),
                                  reads=[ost[h], gt[h]], writes=[ost[h]])
                            kb.op("dve", lambda e: e.tensor_tensor(out=ob[h][:], in0=ost[h][:], in1=sg[h][:], op=ALU.mult),
                                  reads=[ost[h], sg[h]], writes=[ob[h]])
                            kb.dma("pool", mix_d[r0:r0 + 64, h * 128:(h + 1) * 128], ob[h][:], owner=ob[h], reads=[ob[h]])
        kb.barrier()
        rel = [identb, lbc, cm] + zt + qt + e_ + r_ + lf + kk + bb + E1 + E3 + qs + ks + qe + vt + Sf + Sb + PT + ktok + ost
        if bwd:
            rel += [gna] + oft + gt + sg + sq + sm + ob
        kb.release(rel)


def scan_consts():
    cm = np.zeros((128, 640), np.float32)
    s = np.arange(64)[:, None]
    t = np.arange(64)[None, :]
    cm[0:64, 0:64] = (s <= t)
    cm[0:64, 64:128] = (s >= t)
    sm = np.ones(512, np.float32)
    sm[::64] = 0.0
    cm[:, 128:640] = sm[None, :]
    return cm


def phase_mlstm(kb, NSEQ, S, qkT_d, v_d, bo_d, bg_d, cw_d, gb_d, gnb_d, ofw_d, mix_d, cm_d, ones_d, identb_d, bwd):
    TB = 512
    NCH = TB // 64
    NB = S // TB
    H = 4
    dr = 1 if bwd else 0
    with ExitStack() as st:
        identb = const_load(kb, st, "identb", identb_d, [128, 128], BF16)
        cm = const_load(kb, st, "cm", cm_d, [128, 640])
        ones = const_load(kb, st, "ones", ones_d, [64, 128])
        cw = const_load(kb, st, "cw", cw_d, [128, 40])
        msk = cm[0:64, 64:128] if bwd else cm[0:64, 0:64]
        gbt = kb.tile(st, "gbt", [64, 16], F32)
        kb.dma("sp", gbt[:], gb_d.partition_broadcast(64), owner=gbt, writes=[gbt])
        epsT = kb.tile(st, "epsT", [64, 1], F32)
        kb.op("dve", lambda e: e.memset(epsT[:], GN_EPS), writes=[epsT])
        gnb = None
        if bwd:
            gnb = kb.tile(st, "gnb", [64, 512], F32)
            kb.dma("sp", gnb[:], gnb_d.partition_broadcast(64), owner=gnb, writes=[gnb])
        xin = [kb.tile(st, "xin", [128, TB + 4], F32) for _ in range(2)]
        acc = [kb.tile(st, "acc", [128, TB], F32) for _ in range(2)]
        ex = [kb.tile(st, "ex", [128, TB], F32) for _ in range(2)]
        qT = [kb.tile(st, "qT", [128, TB], BF16) for _ in range(H)]
        kT = [kb.tile(st, "kT", [128, TB], BF16) for _ in range(H)]
        va = [kb.tile(st, "va", [64, NCH, 130], BF16) for _ in range(H)]
        for h in range(H):
            kb.op("dve", lambda e: e.memset(va[h][:], 1.0), writes=[va[h]])
        gt = kb.tile(st, "gt", [64, NCH, 16], F32)
        nl = kb.tile(st, "nl", [64, NCH, 4], F32)
        nbT = kb.tile(st, "nbT", [64, NCH, 4], F32)
        wq = kb.tile(st, "wq", [64, NCH, 4], F32)
        wk = kb.tile(st, "wk", [64, NCH, 4], F32)
        wS = kb.tile(st, "wS", [64, NCH, 4], F32)
        dcy = kb.tile(st, "dcy", [128, NCH, 4], F32)
        Cf = [kb.tile(st, "Cf", [128, 130], F32) for _ in range(H)]
        Cb = [kb.tile(st, "Cb", [128, 130], BF16) for _ in range(H)]
        PT = [kb.tile(st, "PT", [64, 64], BF16) for _ in range(H)]
        kh = [kb.tile(st, "kh", [64, 128], BF16) for _ in range(H)]
        ost = [kb.tile(st, "ost", [64, 128], F32) for _ in range(H)]
        sm = [kb.tile(st, "sm", [64, 12], F32) for _ in range(H)]
        if bwd:
            oft = [kb.tile(st, "oft", [64, NCH, 128], F32) for _ in range(H)]
            bot = [kb.tile(st, "bot", [64, NCH, 128], F32) for _ in range(H)]
            sg = [kb.tile(st, "sg", [64, 128], F32) for _ in range(H)]
            sq = [kb.tile(st, "sq", [64, 128], F32) for _ in range(H)]
            ob = [kb.tile(st, "ob", [64, 128], BF16) for _ in range(H)]
        bk = 0
        xi = 0
        for sqi in range(NSEQ):
            for h in range(H):
                kb.op("dve", lambda e: e.memset(Cf[h][:], 0.0), writes=[Cf[h]])
                kb.op("pool", lambda e: e.memset(Cb[h][:], 0.0), writes=[Cb[h]])
            for bi in (range(NB - 1, -1, -1) if bwd else range(NB)):
                tb0 = bi * TB
                t0 = sqi * S + tb0
                for h in range(H):
                    for which in range(2):
                        x = xin[xi % 2]
                        a = acc[xi % 2]
                        ee = ex[xi % 2]
                        xi += 1
                        ch = which * 4 + h
                        lo = max(tb0 - 2, 0)
                        hi = min(tb0 + TB + 2, S)
                        if lo != tb0 - 2 or hi != tb0 + TB + 2:
                            kb.op("dve", lambda e: e.memset(x[:], 0.0), writes=[x])
                        kb.dma("sp", x[:, lo - (tb0 - 2):hi - (tb0 - 2)], qkT_d[ch, :, sqi * S + lo:sqi * S + hi], owner=x, writes=[x])
                        kb.op("dve", lambda e: e.tensor_scalar(out=a[:], in0=x[:, 0:TB], scalar1=cw[:, ch * 5:ch * 5 + 1], scalar2=None,
                                                               op0=ALU.mult), reads=[x, cw], writes=[a])
                        for j in range(1, 5):
                            kb.op("dve", lambda e: e.scalar_tensor_tensor(out=a[:], in0=x[:, j:j + TB], scalar=cw[:, ch * 5 + j:ch * 5 + j + 1],
                                                                          in1=a[:], op0=ALU.mult, op1=ALU.add),
                                  reads=[x, cw, a], writes=[a])
                        kb.op("act", lambda e: e.activation(out=ee[:], in_=a[:], func=AF.Exp, scale=-1.0), reads=[a], writes=[ee])
                        kb.op("dve", lambda e: e.tensor_scalar(out=ee[:], in0=ee[:], scalar1=1.0, scalar2=None, op0=ALU.add),
                              reads=[ee], writes=[ee])
                        kb.op("dve", lambda e: e.reciprocal(out=ee[:], in_=ee[:]), reads=[ee], writes=[ee])
                        if which == 0:
                            kb.op("dve", lambda e: e.tensor_tensor(out=qT[h][:], in0=a[:], in1=ee[:], op=ALU.mult),
                                  reads=[a, ee], writes=[qT[h]])
                        else:
                            kb.op("dve", lambda e: e.scalar_tensor_tensor(out=kT[h][:], in0=a[:], scalar=128.0 ** -0.5, in1=ee[:],
                                                                          op0=ALU.mult, op1=ALU.mult), reads=[a, ee], writes=[kT[h]])
                    kb.dma("sp", va[h][:, :, 0:128], v_d[t0:t0 + TB, h * 128:(h + 1) * 128].rearrange("(c p) f -> p c f", p=64),
                           owner=va[h], writes=[va[h]])
                    if bwd:
                        kb.dma("sp", oft[h][:], ofw_d[t0:t0 + TB, 512 + h * 128:512 + (h + 1) * 128].rearrange("(c p) f -> p c f", p=64),
                               owner=oft[h], writes=[oft[h]])
                        kb.dma("sp", bot[h][:], bo_d[t0:t0 + TB, h * 128:(h + 1) * 128].rearrange("(c p) f -> p c f", p=64),
                               owner=bot[h], writes=[bot[h]])
                kb.dma("sp", gt[:], bg_d[t0:t0 + TB, :].rearrange("(c p) f -> p c f", p=64), owner=gt, writes=[gt])
                kb.op("dve", lambda e: e.tensor_tensor(out=gt[:], in0=gt[:], in1=gbt[:].unsqueeze(1).to_broadcast([64, NCH, 16]),
                                                       op=ALU.add), reads=[gt, gbt], writes=[gt])
                li = gt[:, :, dr * 4:dr * 4 + 4]
                fp = gt[:, :, 8 + dr * 4:8 + dr * 4 + 4]
                kb.op("act", lambda e: e.activation(out=nl[:], in_=fp, func=AF.Exp, scale=-1.0), reads=[gt], writes=[nl])
                kb.op("dve", lambda e: e.tensor_scalar(out=nl[:], in0=nl[:], scalar1=1.0, scalar2=None, op0=ALU.add), reads=[nl], writes=[nl])
                kb.op("act", lambda e: e.activation(out=nl[:], in_=nl[:], func=AF.Ln), reads=[nl], writes=[nl])
                gb_ = bk % 2
                bk += 1
                nl2 = nl[:].rearrange("p c h -> p (c h)")
                kb.op("pe", lambda e: e.matmul(kb.ps[0:64, gb_, 0:NCH * 4], lhsT=msk, rhs=nl2, start=True, stop=True),
                      reads=[cm, nl], writes=[kb.bank[gb_]])
                kb.op("pe", lambda e: e.matmul(kb.ps[:, gb_, 64:64 + NCH * 4], lhsT=ones[:], rhs=nl2, start=True, stop=True),
                      reads=[ones, nl], writes=[kb.bank[gb_]])
                nb_ps = kb.ps[0:64, gb_, 0:NCH * 4].rearrange("p (c h) -> p c h", h=4)
                nT_ps = kb.ps[:, gb_, 64:64 + NCH * 4].rearrange("p (c h) -> p c h", h=4)
                kb.op("act", lambda e: e.activation(out=wq[:], in_=nb_ps, func=AF.Exp, scale=-1.0), reads=[kb.bank[gb_]], writes=[wq])
                kb.op("dve", lambda e: e.tensor_tensor(out=nbT[:], in0=nb_ps, in1=li, op=ALU.add), reads=[kb.bank[gb_], gt], writes=[nbT])
                kb.op("act", lambda e: e.activation(out=wk[:], in_=nbT[:], func=AF.Exp), reads=[nbT], writes=[wk])
                kb.op("dve", lambda e: e.tensor_tensor(out=nbT[:], in0=nbT[:], in1=nT_ps[0:64], op=ALU.subtract),
                      reads=[kb.bank[gb_], nbT], writes=[nbT])
                kb.op("act", lambda e: e.activation(out=wS[:], in_=nbT[:], func=AF.Exp), reads=[nbT], writes=[wS])
                kb.op("act", lambda e: e.activation(out=dcy[:], in_=nT_ps, func=AF.Exp, scale=-1.0), reads=[kb.bank[gb_]], writes=[dcy])
                for ci in (range(NCH - 1, -1, -1) if bwd else range(NCH)):
                    cs = slice(ci * 64, ci * 64 + 64)
                    for h in range(H):
                        b1 = 2 + (bk % 2)
                        b3_ = 4 + (bk % 2)
                        b4 = 6 + (bk % 2)
                        bk += 1
                        kb.op("pe", lambda e: e.matmul(kb.ps[0:64, b1, 0:64], lhsT=kT[h][:, cs], rhs=qT[h][:, cs],
                                                       start=True, stop=True), reads=[kT[h], qT[h]], writes=[kb.bank[b1]])
                        kb.op("dve", lambda e: e.scalar_tensor_tensor(out=PT[h][:], in0=kb.ps[0:64, b1, 0:64], scalar=wk[:, ci, h:h + 1],
                                                                      in1=msk, op0=ALU.mult, op1=ALU.mult),
                              reads=[kb.bank[b1], wk, cm], writes=[PT[h]])
                        pst = kb.ps[:, b1, :].bitcast(BF16)
                        kb.op("pe", lambda e: e.transpose(out=pst[0:64, 256:384], in_=kT[h][:, cs], identity=identb[:]),
                              reads=[kT[h], identb], writes=[kb.bank[b1]])
                        kb.op("act", lambda e: e.activation(out=kh[h][:], in_=pst[0:64, 256:384], func=AF.Identity,
                                                            scale=wS[:, ci, h:h + 1]), reads=[kb.bank[b1], wS], writes=[kh[h]])
                        kb.op("pe", lambda e: e.matmul(kb.ps[0:64, b3_, 0:129], lhsT=PT[h][:], rhs=va[h][:, ci, 0:129],
                                                       start=True, stop=False), reads=[PT[h], va[h]], writes=[kb.bank[b3_]])
                        kb.op("pe", lambda e: e.matmul(kb.ps[0:64, b3_, 0:129], lhsT=qT[h][:, cs], rhs=Cb[h][:, 0:129],
                                                       start=False, stop=True), reads=[qT[h], Cb[h]], writes=[kb.bank[b3_]])
                        kb.op("pe", lambda e: e.matmul(kb.ps[:, b4, 0:129], lhsT=kh[h][:], rhs=va[h][:, ci, 0:129],
                                                       start=True, stop=True), reads=[kh[h], va[h]], writes=[kb.bank[b4]])
                        kb.op("dve", lambda e: e.scalar_tensor_tensor(out=Cf[h][:, 0:129], in0=Cf[h][:, 0:129], scalar=dcy[:, ci, h:h + 1],
                                                                      in1=kb.ps[:, b4, 0:129], op0=ALU.mult, op1=ALU.add),
                              reads=[Cf[h], dcy, kb.bank[b4]], writes=[Cf[h]])
                        kb.op("act", lambda e: e.copy(out=Cb[h][:], in_=Cf[h][:]), reads=[Cf[h]], writes=[Cb[h]])
                        s_ = sm[h]
                        wqc = wq[:, ci, h:h + 1]
                        kb.op("dve", lambda e: e.tensor_scalar(out=s_[:, 0:1], in0=kb.ps[0:64, b3_, 128:129], scalar1=wqc, scalar2=None,
                                                               op0=ALU.mult), reads=[kb.bank[b3_], wq], writes=[s_])
                        kb.op("dve", lambda e: e.scalar_tensor_tensor(out=s_[:, 7:8], in0=s_[:, 0:1], scalar=-1.0, in1=s_[:, 0:1],
                                                                      op0=ALU.mult, op1=ALU.max), reads=[s_], writes=[s_])
                        kb.op("dve", lambda e: e.tensor_scalar(out=s_[:, 0:1], in0=s_[:, 7:8], scalar1=1.0, scalar2=None,
                                                               op0=ALU.max), reads=[s_], writes=[s_])
                        kb.op("dve", lambda e: e.reciprocal(out=s_[:, 0:1], in_=s_[:, 0:1]), reads=[s_], writes=[s_])
                        kb.op("dve", lambda e: e.tensor_tensor(out=s_[:, 1:2], in0=s_[:, 0:1], in1=wqc, op=ALU.mult),
                              reads=[s_, wq], writes=[s_])
                        r0 = t0 + ci * 64
                        cc = slice(512 + h * 128, 512 + (h + 1) * 128)
                        if not bwd:
                            kb.op("act", lambda e: e.activation(out=ost[h][:], in_=kb.ps[0:64, b3_, 0:128], func=AF.Identity,
                                                                scale=s_[:, 1:2]), reads=[kb.bank[b3_], s_], writes=[ost[h]])
                            kb.dma("pool", ofw_d[r0:r0 + 64, cc], ost[h][:], owner=ost[h], reads=[ost[h]])
                        else:
                            kb.op("dve", lambda e: e.scalar_tensor_tensor(out=ost[h][:], in0=kb.ps[0:64, b3_, 0:128], scalar=s_[:, 1:2],
                                                                          in1=oft[h][:, ci, :], op0=ALU.mult, op1=ALU.add),
                                  reads=[kb.bank[b3_], s_, oft[h]], writes=[ost[h]])
                            kb.op("act", lambda e: e.activation(out=sq[h][:], in_=ost[h][:], func=AF.Identity, accum_out=s_[:, 2:3]),
                                  reads=[ost[h]], writes=[sq[h], s_])
                            kb.op("dve", lambda e: e.tensor_scalar(out=s_[:, 3:4], in0=s_[:, 2:3], scalar1=-1.0 / 128, scalar2=None,
                                                                   op0=ALU.mult), reads=[s_], writes=[s_])
                            kb.op("dve", lambda e: e.tensor_scalar(out=ost[h][:], in0=ost[h][:], scalar1=s_[:, 3:4], scalar2=None,
                                                                   op0=ALU.add), reads=[ost[h], s_], writes=[ost[h]])
                            kb.op("act", lambda e: e.activation(out=sq[h][:], in_=ost[h][:], func=AF.Square, accum_out=s_[:, 4:5]),
                                  reads=[ost[h]], writes=[sq[h], s_])
                            kb.op("act", lambda e: e.activation(out=s_[:, 5:6], in_=s_[:, 4:5], func=AF.Ln, scale=1.0 / 128,
                                                                bias=epsT[:]), reads=[s_, epsT], writes=[s_])
                            kb.op("act", lambda e: e.activation(out=s_[:, 6:7], in_=s_[:, 5:6], func=AF.Exp, scale=-0.5),
                                  reads=[s_], writes=[s_])
                            kb.op("act", lambda e: e.activation(out=sg[h][:], in_=bot[h][:, ci, :], func=AF.Exp, scale=-1.0),
                                  reads=[bot[h]], writes=[sg[h]])
                            kb.op("dve", lambda e: e.tensor_scalar(out=sg[h][:], in0=sg[h][:], scalar1=1.0, scalar2=None, op0=ALU.add),
                                  reads=[sg[h]], writes=[sg[h]])
                            kb.op("dve", lambda e: e.reciprocal(out=sg[h][:], in_=sg[h][:]), reads=[sg[h]], writes=[sg[h]])
                            kb.op("dve", lambda e: e.scalar_tensor_tensor(out=ost[h][:], in0=ost[h][:], scalar=s_[:, 6:7],
                                                                          in1=gnb[:, h * 128:(h + 1) * 128], op0=ALU.mult, op1=ALU.mult),
                                  reads=[ost[h], s_, gnb], writes=[ost[h]])
                            kb.op("dve", lambda e: e.tensor_tensor(out=ob[h][:], in0=ost[h][:], in1=sg[h][:], op=ALU.mult),
                                  reads=[ost[h], sg[h]], writes=[ob[h]])
                            kb.dma("pool", mix_d[r0:r0 + 64, cc], ob[h][:], owner=ob[h], reads=[ob[h]])
        kb.barrier()
        rel = [identb, cm, ones, cw, gbt, epsT, gt, nl, nbT, wq, wk, wS, dcy] + xin + acc + ex + qT + kT + va + Cf + Cb + PT + kh + ost + sm
        if bwd:
            rel += [gnb] + oft + bot + sg + sq + ob
        kb.release(rel)


def phase_lbprep(kb, lbr_d, lbt_d):
    with ExitStack() as st:
        tiles = []
        for d in range(2):
            r0 = kb.tile(st, "r0", [128, 4], F32)
            r1 = kb.tile(st, "r1", [128, 4], F32)
            T0 = kb.tile(st, "T0", [128, 10], F32)
            T1 = kb.tile(st, "T1", [128, 10], F32)
            tiles += [r0, r1, T0, T1]
            kb.dma("sp", r0[:], lbr_d[d, 0], owner=r0, writes=[r0])
            kb.dma("sp", r1[:], lbr_d[d, 1], owner=r1, writes=[r1])
            kb.op("dve", lambda e: e.memset(T0[:], 0.0), writes=[T0])
            kb.op("dve", lambda e: e.memset(T0[:, 0:8].rearrange("p (h t) -> p h t", t=2)[:, :, 1:2], 1.0), reads=[T0], writes=[T0])
            kb.op("dve", lambda e: e.memset(T0[:, 8:9], GN_EPS), reads=[T0], writes=[T0])
            kb.op("dve", lambda e: e.tensor_tensor(out=r0[:], in0=r0[:], in1=r1[:], op=ALU.subtract), reads=[r0, r1], writes=[r0])
            kb.op("act", lambda e: e.activation(out=r0[:], in_=r0[:], func=AF.Exp), reads=[r0], writes=[r0])
            kb.op("dve", lambda e: e.tensor_scalar(out=r1[:], in0=r0[:], scalar1=1.0, scalar2=None, op0=ALU.add), reads=[r0], writes=[r1])
            kb.op("dve", lambda e: e.reciprocal(out=r1[:], in_=r1[:]), reads=[r1], writes=[r1])
            kb.op("dve", lambda e: e.memset(T1[:], GN_EPS), writes=[T1])
            v1 = T1[:, 0:8].rearrange("p (h t) -> p h t", t=2)
            kb.op("dve", lambda e: e.tensor_copy(out=v1[:, :, 0:1], in_=r1[:].unsqueeze(2)), reads=[r1, T1], writes=[T1])
            kb.op("dve", lambda e: e.tensor_tensor(out=v1[:, :, 1:2], in0=r0[:].unsqueeze(2), in1=r1[:].unsqueeze(2), op=ALU.mult),
                  reads=[r0, r1, T1], writes=[T1])
            kb.dma("pool", lbt_d[0, d], T0[:], owner=T0, reads=[T0])
            kb.dma("pool", lbt_d[1, d], T1[:], owner=T1, reads=[T1])
        kb.barrier()
        kb.release(tiles)


IN_SPECS = [("x", None), ("w_in_even", [2, D, 4624]), ("w_out_even", [2, D, D]), ("w_qkv_odd", [2, D, 3072]),
            ("w_out_odd", [2, D, D]), ("w_ffn_gate", [4, D, FF]), ("w_ffn_up", [4, D, FF]), ("w_ffn_down", [4, FF, D]),
            ("ln_mix_g", [4, D]), ("ln_mix_b", [4, D]), ("ln_ffn_g", [4, D]), ("ln_ffn_b", [4, D]),
            ("gate_bias_even", [2, 16]), ("gn_hgrn", [2, 512]), ("gn_mlstm", [2, 512]),
            ("lbr", [2, 2, 128, 4]), ("cwl", [2, 128, 40]), ("G", [2, 128, 32 * 14 * 64]),
            ("ident", [128, 128]), ("negm", [128, 64]), ("cm", [128, 640]), ("ones", [64, 128])]


def build_program(NSEQ, S, depth=4):
    NT = NSEQ * S
    nc = bass.Bass("TRN2", target_bir_lowering=False)
    I = {}
    for name, shp in IN_SPECS:
        if shp is None:
            shp = [NT, D]
        I[name] = nc.dram_tensor(name, shp, F32, kind="ExternalInput").ap()
    out = nc.dram_tensor("out", [NT, D], F32, kind="ExternalOutput").ap()

    def scr(name, shp, dt):
        return nc.dram_tensor(name, shp, dt, kind="Internal").ap()
    aqT = scr("aqT", [4, 128, NT], F32)
    zfT = scr("zfT", [4, 128, NT], F32)
    zbT = scr("zbT", [4, 128, NT], F32)
    ai = scr("ai", [NT, 512], BF16)
    ag = scr("ag", [NT, 512], F32)
    bqkT = scr("bqkT", [8, 128, NT], F32)
    bv = scr("bv", [NT, 512], BF16)
    bo = scr("bo", [NT, 512], F32)
    bg = scr("bg", [NT, 16], F32)
    ofw = scr("ofw", [NT, 1024], F32)
    mix = scr("mix", [NT, 1024], BF16)
    qT = scr("qTs", [11, 128, NT], BF16)
    kT = scr("kTs", [11, 128, NT], BF16)
    vv = scr("vs", [NT, 1024], BF16)
    lbt = scr("lbt", [2, 2, 128, 10], F32)
    with ExitStack() as stack:
        kb = KB(nc, stack)
        phase_lbprep(kb, I["lbr"], lbt)
        hsrc = I["x"]
        for l in range(depth):
            j = l // 2
            if l % 2 == 0:
                outs = [dict(kind="F", c0=0, n=512, dram=aqT, dtype=F32),
                        dict(kind="F", c0=512, n=512, dram=zfT, dtype=F32),
                        dict(kind="F", c0=1024, n=512, dram=zbT, dtype=F32),
                        dict(kind="T", c0=1536, n=512, dram=ai, dtype=BF16),
                        dict(kind="T", c0=2048, n=512, dram=ag, dtype=F32),
                        dict(kind="F", c0=2560, n=1024, dram=bqkT, dtype=F32),
                        dict(kind="T", c0=3584, n=512, dram=bv, dtype=BF16),
                        dict(kind="T", c0=4096, n=512, dram=bo, dtype=F32),
                        dict(kind="T", c0=4608, n=16, dram=bg, dtype=F32)]
                phase_proj(kb, NT, hsrc, I["w_in_even"][j], 4624, outs, I["ident"])
                phase_hgrn2(kb, NSEQ, S, aqT, zfT, ai, ag, lbt[j, 0], I["gn_hgrn"][j], ofw, mix, I["cm"], I["ident"], bwd=False)
                phase_hgrn2(kb, NSEQ, S, aqT, zbT, ai, ag, lbt[j, 1], I["gn_hgrn"][j], ofw, mix, I["cm"], I["ident"], bwd=True)
                phase_mlstm(kb, NSEQ, S, bqkT, bv, bo, bg, I["cwl"][j], I["gate_bias_even"][j], I["gn_mlstm"][j], ofw, mix,
                            I["cm"], I["ones"], I["ident"], bwd=False)
                phase_mlstm(kb, NSEQ, S, bqkT, bv, bo, bg, I["cwl"][j], I["gate_bias_even"][j], I["gn_mlstm"][j], ofw, mix,
                            I["cm"], I["ones"], I["ident"], bwd=True)
                w_o = I["w_out_even"][j]
            else:
                outs = [dict(kind="F", c0=0, n=1024, dram=qT, dtype=BF16, scale=32.0 ** -0.5, fw=96),
                        dict(kind="F", c0=1024, n=1024, dram=kT, dtype=BF16, fw=96),
                        dict(kind="T", c0=2048, n=1024, dram=vv, dtype=BF16)]
                phase_proj(kb, NT, hsrc, I["w_qkv_odd"][j], 3072, outs, I["ident"])
                phase_na(kb, NSEQ, S, qT, kT, vv, I["G"][j], I["negm"], mix)
                w_o = I["w_out_odd"][j]
            phase_m3(kb, NT, mix, hsrc, w_o, I["ln_mix_g"][l], I["ln_mix_b"][l], I["ident"], hout_dram=out)
            hsrc = out
            phase_ffn(kb, NT, out, I["w_ffn_gate"][l], I["w_ffn_up"][l], I["w_ffn_down"][l], I["ln_ffn_g"][l], I["ln_ffn_b"][l],
                      I["ident"])
        kb.final_wait()
        n_ins = kb.n_ins
    return nc, n_ins


def host_inputs(inp):
    f = lambda a: np.ascontiguousarray(np.asarray(a, dtype=np.float32))
    sh = {k: f(inp[k]) for k in ["w_in_even", "w_out_even", "w_qkv_odd", "w_out_odd", "w_ffn_gate", "w_ffn_up", "w_ffn_down",
                                 "ln_mix_g", "ln_mix_b", "ln_ffn_g", "ln_ffn_b", "gate_bias_even", "gn_hgrn", "gn_mlstm"]}
    lb_raw = f(inp["lb_raw"])
    sh["lbr"] = np.ascontiguousarray(lb_raw.reshape(2, 2, 4, 128).transpose(0, 1, 3, 2))
    conv = f(inp["conv_qk"])
    sh["cwl"] = np.ascontiguousarray(conv.transpose(0, 2, 1).reshape(2, 8, 128, 5).transpose(0, 2, 1, 3).reshape(2, 128, 40))
    rpb = f(inp["rpb_odd"])
    sh["G"] = np.stack([na_host_tables(rpb[j]) for j in range(rpb.shape[0])])
    sh["ident"] = np.eye(128, dtype=np.float32)
    sh["negm"] = na_const_mask()
    sh["cm"] = scan_consts()
    sh["ones"] = np.ones((64, 128), np.float32)
    return sh


def kernel(**inputs):
    x = np.asarray(inputs["x"], dtype=np.float32)
    B, S, _ = x.shape
    ncores = 8
    NSEQ = B // ncores
    nc, _ = build_program(NSEQ, S)
    sh = host_inputs(inputs)
    in_maps = []
    for c in range(ncores):
        m = dict(sh)
        m["x"] = np.ascontiguousarray(x[c * NSEQ:(c + 1) * NSEQ].reshape(NSEQ * S, D))
        in_maps.append(m)
    res = run_bass_kernel_spmd(nc, in_maps, core_ids=list(range(ncores)))
    outs = [np.asarray(r["out"], dtype=np.float32).reshape(NSEQ, S, D) for r in res.results]
    return np.concatenate(outs, axis=0)
```
